# Optimizing a Trainium2 kernel written in Bass

```python
import math
import jax, jax.numpy as jnp
from jax import lax
import numpy as np

D_MODEL = 1024
BATCH = 32
SEQ = 2048
DEPTH = 2
DEC_BATCH = 8
DEC_SEQ = 4096
PAST_LEN = 128

ATT_HEADS = 8
ATT_HEAD_DIM = 64
ATT_V_DIM = 2 * ATT_HEAD_DIM
ATT_WIDTH = ATT_HEADS * ATT_V_DIM
QBLOCK = 128
NUM_BUCKETS = 32
MAX_DISTANCE = 128
SSM_INNER = 1024
SSM_HEAD_DIM = 64
SSM_HEADS = SSM_INNER // SSM_HEAD_DIM
SSM_GROUPS = 2
SSM_STATE = 128
D_CONV = 5
CHUNK = 128
CONV_CH = SSM_INNER + 2 * SSM_GROUPS * SSM_STATE
D_FF = ((8 * D_MODEL // 3 + 255) // 256) * 256
Q_COLS = ATT_HEADS * 2 * ATT_HEAD_DIM
K_COLS = ATT_HEADS * 2 * ATT_HEAD_DIM
V_COLS = ATT_WIDTH
Z_COLS = SSM_INNER
XBC_COLS = CONV_CH
DT_COLS = 2 * SSM_HEADS
GATE_COLS = 2 * D_MODEL
IN_COLS = Q_COLS + K_COLS + V_COLS + Z_COLS + XBC_COLS + DT_COLS + GATE_COLS
SPLIT_POINTS = (Q_COLS, Q_COLS + K_COLS, Q_COLS + K_COLS + V_COLS,
                Q_COLS + K_COLS + V_COLS + Z_COLS,
                Q_COLS + K_COLS + V_COLS + Z_COLS + XBC_COLS,
                Q_COLS + K_COLS + V_COLS + Z_COLS + XBC_COLS + DT_COLS)
EPS = 1e-6

kernel_name = "hybrid_diffattn_ssd_encoder"


def rmsnorm(x, w):
    x32 = x.astype(jnp.float32)
    y = x32 * lax.rsqrt(jnp.mean(x32 * x32, axis=-1, keepdims=True) + EPS)
    return (y * w.astype(jnp.float32)).astype(x.dtype)


def rel_bucket(rel):
    nb = NUM_BUCKETS // 2
    ret = (rel > 0).astype(jnp.int32) * nb
    n = jnp.abs(rel)
    max_exact = nb // 2
    nf = jnp.maximum(n, 1).astype(jnp.float32)
    large = max_exact + (jnp.log(nf / max_exact) / math.log(MAX_DISTANCE / max_exact)
                         * (nb - max_exact)).astype(jnp.int32)
    large = jnp.minimum(large, nb - 1)
    return ret + jnp.where(n < max_exact, n, large)


def diff_attention(q, k, v, lam, rel_bias):
    b, S = q.shape[0], q.shape[1]
    nb = S // QBLOCK
    scale = ATT_HEAD_DIM ** -0.5
    qb = jnp.moveaxis(q.reshape(b, nb, QBLOCK, ATT_HEADS, 2, ATT_HEAD_DIM), 1, 0)
    starts = jnp.arange(nb, dtype=jnp.int32) * QBLOCK
    k_pos = jnp.arange(S, dtype=jnp.int32)
    kf = k.astype(jnp.float32)
    vf = v.astype(jnp.float32)
    table = rel_bias.astype(jnp.float32)

    def block(args):
        qblk, start = args
        q_pos = start + jnp.arange(QBLOCK, dtype=jnp.int32)
        bias = table[rel_bucket(k_pos[None, :] - q_pos[:, None])]
        bias = jnp.transpose(bias, (2, 0, 1))
        s = jnp.einsum('bqhmd,bkhmd->bhmqk', qblk.astype(jnp.float32), kf) * scale
        p = jax.nn.softmax(s + bias[None, :, None], axis=-1)
        a = p[:, :, 0] - lam * p[:, :, 1]
        return jnp.einsum('bhqk,bkhe->bqhe', a, vf)

    out = lax.map(block, (qb, starts))
    return jnp.moveaxis(out, 0, 1).reshape(b, S, ATT_HEADS, ATT_V_DIM)


def ssd_scan(x, dt, A, Bm, Cm):
    b, L, h, p = x.shape
    g, n = Bm.shape[2], Bm.shape[3]
    r = h // g
    c = L // CHUNK
    f32 = jnp.float32
    dt = dt.astype(f32)
    xdt = (x.astype(f32) * dt[..., None]).reshape(b, c, CHUNK, g, r, p)
    a = (dt * A.astype(f32)).reshape(b, c, CHUNK, g, r).transpose(0, 3, 4, 1, 2)
    Bc = Bm.astype(f32).reshape(b, c, CHUNK, g, n)
    Cc = Cm.astype(f32).reshape(b, c, CHUNK, g, n)
    a_cum = jnp.cumsum(a, axis=-1)
    seg = a_cum[..., :, None] - a_cum[..., None, :]
    tri = jnp.tril(jnp.ones((CHUNK, CHUNK), dtype=bool))
    Lmat = jnp.exp(jnp.where(tri, seg, -jnp.inf))
    CB = jnp.einsum('bclgn,bcsgn->bgcls', Cc, Bc)
    y_diag = jnp.einsum('bgrcls,bcsgrp->bclgrp', CB[:, :, None] * Lmat, xdt)
    decay_states = jnp.exp(a_cum[..., -1:] - a_cum)
    states = jnp.einsum('bclgn,bgrcl,bclgrp->bcgrpn', Bc, decay_states, xdt)
    chunk_decay = jnp.exp(a_cum[..., -1])

    def step(carry, inp):
        s_c, d_c = inp
        return carry * d_c[..., None, None] + s_c, carry

    init = jnp.zeros((b, g, r, p, n), f32)
    _, prev = lax.scan(step, init, (jnp.moveaxis(states, 1, 0), jnp.moveaxis(chunk_decay, -1, 0)))
    prev = jnp.moveaxis(prev, 0, 1)
    y_off = jnp.einsum('bclgn,bcgrpn,bgrcl->bclgrp', Cc, prev, jnp.exp(a_cum))
    return (y_diag + y_off).reshape(b, L, h, p)


def mixer(h, li, rel_bias, w_in, lambda_q1, lambda_k1, lambda_q2, lambda_k2, subln_w,
          conv_w, conv_b, dt_bias_f, dt_bias_b, a_log_f, a_log_b, d_skip, ssm_norm_w,
          w_proj_attn, w_proj_ssm, w_out):
    b, S, _ = h.shape
    proj = h @ w_in
    q, k, v, z, xbc, dt_raw, gates = jnp.split(proj, SPLIT_POINTS, axis=-1)
    q = q.reshape(b, S, ATT_HEADS, 2, ATT_HEAD_DIM)
    k = k.reshape(b, S, ATT_HEADS, 2, ATT_HEAD_DIM)
    v = v.reshape(b, S, ATT_HEADS, ATT_V_DIM)
    lam_init = 0.8 - 0.6 * math.exp(-0.3 * li)
    f32 = jnp.float32
    lam = (jnp.exp(jnp.sum(lambda_q1.astype(f32) * lambda_k1.astype(f32)))
           - jnp.exp(jnp.sum(lambda_q2.astype(f32) * lambda_k2.astype(f32))) + lam_init)
    att = diff_attention(q, k, v, lam, rel_bias)
    att = rmsnorm(att, subln_w) * (1.0 - lam_init)
    att = att.reshape(b, S, ATT_WIDTH).astype(h.dtype)
    xbc = lax.conv_general_dilated(xbc, conv_w, window_strides=(1,),
                                   padding=[(D_CONV // 2, D_CONV // 2)],
                                   dimension_numbers=('NWC', 'WIO', 'NWC'),
                                   feature_group_count=CONV_CH)
    xbc = jax.nn.silu(xbc + conv_b)
    xs, Bm, Cm = jnp.split(xbc, (SSM_INNER, SSM_INNER + SSM_GROUPS * SSM_STATE), axis=-1)
    xs = xs.reshape(b, S, SSM_HEADS, SSM_HEAD_DIM)
    Bm = Bm.reshape(b, S, SSM_GROUPS, SSM_STATE)
    Cm = Cm.reshape(b, S, SSM_GROUPS, SSM_STATE)
    dt_f = jax.nn.softplus((dt_raw[..., :SSM_HEADS] + dt_bias_f).astype(f32))
    dt_b = jax.nn.softplus((dt_raw[..., SSM_HEADS:] + dt_bias_b).astype(f32))
    A_f = -jnp.exp(a_log_f.astype(f32))
    A_b = -jnp.exp(a_log_b.astype(f32))
    y_f = ssd_scan(xs, dt_f, A_f, Bm, Cm)
    y_b = jnp.flip(ssd_scan(jnp.flip(xs, 1), jnp.flip(dt_b, 1), A_b,
                            jnp.flip(Bm, 1), jnp.flip(Cm, 1)), 1)
    y = y_f + y_b + d_skip.astype(f32)[:, None] * xs.astype(f32)
    y = y.reshape(b, S, SSM_INNER) * jax.nn.silu(z.astype(f32))
    y_ssm = rmsnorm(y, ssm_norm_w).astype(h.dtype)
    g = jax.nn.sigmoid(gates.reshape(b, S, 2, D_MODEL))
    merged = g[:, :, 0] * (att @ w_proj_attn) + g[:, :, 1] * (y_ssm @ w_proj_ssm)
    return merged @ w_out


def trunk(x, rel_bias, norm_pre_mix, w_in, lambda_q1, lambda_k1, lambda_q2, lambda_k2,
          subln_w, conv_w, conv_b, dt_bias_f, dt_bias_b, a_log_f, a_log_b, d_skip,
          ssm_norm_w, w_proj_attn, w_proj_ssm, w_out, norm_post_mix, norm_pre_ffn,
          w_gate_up, w_down, norm_post_ffn):
    for l in range(DEPTH):
        h = rmsnorm(x, norm_pre_mix[l])
        m = mixer(h, l, rel_bias, w_in[l], lambda_q1[l], lambda_k1[l], lambda_q2[l],
                  lambda_k2[l], subln_w[l], conv_w[l], conv_b[l], dt_bias_f[l], dt_bias_b[l],
                  a_log_f[l], a_log_b[l], d_skip[l], ssm_norm_w[l], w_proj_attn[l],
                  w_proj_ssm[l], w_out[l])
        x = x + rmsnorm(m, norm_post_mix[l])
        h = rmsnorm(x, norm_pre_ffn[l])
        gate, up = jnp.split(h @ w_gate_up[l], 2, axis=-1)
        f = (jax.nn.silu(gate) * up) @ w_down[l]
        x = x + rmsnorm(f, norm_post_ffn[l])
    return x


def setup_inputs(seed: int = 0) -> dict:
    key = jax.random.key(seed)
    ks = jax.random.split(key, 32)
    f32 = jnp.float32

    def nrm(k, shape, scale):
        return jax.random.normal(k, shape, f32) * scale

    def gain(k, shape):
        return 1.0 + 0.05 * jax.random.normal(k, shape, f32)

    dt0 = jnp.exp(jax.random.uniform(ks[10], (2, DEPTH, SSM_HEADS), f32)
                  * (math.log(0.1) - math.log(0.001)) + math.log(0.001))
    dt_bias = dt0 + jnp.log(-jnp.expm1(-dt0))
    a_log = jnp.log(jax.random.uniform(ks[11], (2, DEPTH, SSM_HEADS), f32, 1.0, 16.0))
    return {
        'x_prompt': nrm(ks[0], (BATCH, SEQ, D_MODEL), 1.0),
        'x_sample': nrm(ks[1], (DEC_BATCH, DEC_SEQ, D_MODEL), 1.0),
        'rel_bias': nrm(ks[2], (NUM_BUCKETS, ATT_HEADS), 0.5),
        'norm_pre_mix': gain(ks[3], (DEPTH, D_MODEL)),
        'w_in': nrm(ks[4], (DEPTH, D_MODEL, IN_COLS), D_MODEL ** -0.5),
        'lambda_q1': nrm(ks[5], (DEPTH, ATT_HEAD_DIM), 0.1),
        'lambda_k1': nrm(ks[6], (DEPTH, ATT_HEAD_DIM), 0.1),
        'lambda_q2': nrm(ks[7], (DEPTH, ATT_HEAD_DIM), 0.1),
        'lambda_k2': nrm(ks[8], (DEPTH, ATT_HEAD_DIM), 0.1),
        'subln_w': gain(ks[9], (DEPTH, ATT_V_DIM)),
        'conv_w': nrm(ks[12], (DEPTH, D_CONV, 1, CONV_CH), D_CONV ** -0.5),
        'conv_b': nrm(ks[13], (DEPTH, CONV_CH), 0.01),
        'dt_bias_f': dt_bias[0],
        'dt_bias_b': dt_bias[1],
        'a_log_f': a_log[0],
        'a_log_b': a_log[1],
        'd_skip': gain(ks[14], (DEPTH, SSM_HEADS)),
        'ssm_norm_w': gain(ks[15], (DEPTH, SSM_INNER)),
        'w_proj_attn': nrm(ks[16], (DEPTH, ATT_WIDTH, D_MODEL), ATT_WIDTH ** -0.5),
        'w_proj_ssm': nrm(ks[17], (DEPTH, SSM_INNER, D_MODEL), SSM_INNER ** -0.5),
        'w_out': nrm(ks[18], (DEPTH, D_MODEL, D_MODEL), D_MODEL ** -0.5),
        'norm_post_mix': gain(ks[19], (DEPTH, D_MODEL)),
        'norm_pre_ffn': gain(ks[20], (DEPTH, D_MODEL)),
        'w_gate_up': nrm(ks[21], (DEPTH, D_MODEL, 2 * D_FF), D_MODEL ** -0.5),
        'w_down': nrm(ks[22], (DEPTH, D_FF, D_MODEL), D_FF ** -0.5),
        'norm_post_ffn': gain(ks[23], (DEPTH, D_MODEL)),
    }


def reference(x_prompt, x_sample, rel_bias, norm_pre_mix, w_in, lambda_q1, lambda_k1,
              lambda_q2, lambda_k2, subln_w, conv_w, conv_b, dt_bias_f, dt_bias_b,
              a_log_f, a_log_b, d_skip, ssm_norm_w, w_proj_attn, w_proj_ssm, w_out,
              norm_post_mix, norm_pre_ffn, w_gate_up, w_down, norm_post_ffn):
    y_prompt = trunk(x_prompt, rel_bias, norm_pre_mix, w_in, lambda_q1, lambda_k1, lambda_q2,
                     lambda_k2, subln_w, conv_w, conv_b, dt_bias_f, dt_bias_b, a_log_f,
                     a_log_b, d_skip, ssm_norm_w, w_proj_attn, w_proj_ssm, w_out,
                     norm_post_mix, norm_pre_ffn, w_gate_up, w_down, norm_post_ffn)
    y_sample = trunk(x_sample, rel_bias, norm_pre_mix, w_in, lambda_q1, lambda_k1, lambda_q2,
                     lambda_k2, subln_w, conv_w, conv_b, dt_bias_f, dt_bias_b, a_log_f,
                     a_log_b, d_skip, ssm_norm_w, w_proj_attn, w_proj_ssm, w_out,
                     norm_post_mix, norm_pre_ffn, w_gate_up, w_down, norm_post_ffn)
    return (y_prompt, y_sample)
```

```python
import math
from contextlib import ExitStack

import numpy as np
import concourse.bass as bass
import concourse.mybir as mybir
from concourse.bass_utils import run_bass_kernel_spmd

F32 = mybir.dt.float32
BF16 = mybir.dt.bfloat16
AF = mybir.ActivationFunctionType
ALU = mybir.AluOpType

D = 1024
INC = 7712
DFF = 2816
NHA = 8
EPS = 1e-6
NEG = -30000.0

PARAMS = [
    ("rel_bias", (32, 8)), ("norm_pre_mix", (2, 1024)), ("w_in", (2, 1024, 7712)),
    ("lambda_q1", (2, 64)), ("lambda_k1", (2, 64)), ("lambda_q2", (2, 64)), ("lambda_k2", (2, 64)),
    ("subln_w", (2, 128)), ("conv_w", (2, 5, 1, 1536)), ("conv_b", (2, 1536)),
    ("dt_bias_f", (2, 16)), ("dt_bias_b", (2, 16)), ("a_log_f", (2, 16)), ("a_log_b", (2, 16)),
    ("d_skip", (2, 16)), ("ssm_norm_w", (2, 1024)), ("w_proj_attn", (2, 1024, 1024)),
    ("w_proj_ssm", (2, 1024, 1024)), ("w_out", (2, 1024, 1024)), ("norm_post_mix", (2, 1024)),
    ("norm_pre_ffn", (2, 1024)), ("w_gate_up", (2, 1024, 5632)), ("w_down", (2, 2816, 1024)),
    ("norm_post_ffn", (2, 1024)),
]

GROUPS = ([(c, 512, "q") for c in (0, 512)] + [(c, 512, "k") for c in (1024, 1536)]
          + [(c, 512, "v") for c in (2048, 2560)] + [(c, 512, "z") for c in (3072, 3584)]
          + [(c, 512, "x") for c in (4096, 4608, 5120)] + [(5632, 32, "dt")]
          + [(c, 512, "g") for c in (5664, 6176, 6688, 7200)])


class Buf:
    __slots__ = ("w", "r", "multi")

    def __init__(self, multi=False):
        self.w = {}
        self.r = {}
        self.multi = multi


class Sched:
    def __init__(self, nc, es):
        self.nc = nc
        self.es = es
        self.eng = {"pe": nc.tensor, "act": nc.scalar, "dve": nc.vector, "pool": nc.gpsimd, "sp": nc.sync}
        self.sems = {}
        self.val = {}
        self.waited = {e: {} for e in self.eng}
        for e in self.eng:
            self._sem(e)

    def _sem(self, key):
        if key not in self.sems:
            self.sems[key] = self.es.enter_context(self.nc.semaphore("s%d" % len(self.sems)))
            self.val[key] = 0
        return self.sems[key]

    def _wait(self, e, deps):
        eng = self.eng[e]
        wd = self.waited[e]
        for k, v in deps.items():
            if e == "pe" and k == "pe":
                continue
            if wd.get(k, 0) < v:
                eng.wait_ge(self.sems[k], v)
                wd[k] = v

    @staticmethod
    def _deps(reads, writes):
        deps = {}
        for b in reads:
            for k, v in b.w.items():
                if deps.get(k, 0) < v:
                    deps[k] = v
        for b in writes:
            for d in ((b.r,) if b.multi else (b.w, b.r)):
                for k, v in d.items():
                    if deps.get(k, 0) < v:
                        deps[k] = v
        return deps

    @staticmethod
    def _stamp(reads, writes, k, v):
        for b in reads:
            if b.r.get(k, 0) < v:
                b.r[k] = v
        for b in writes:
            if b.multi:
                if b.w.get(k, 0) < v:
                    b.w[k] = v
            else:
                b.w = {k: v}
                b.r = {}

    def op(self, e, reads, writes, fn, signal=True):
        self._wait(e, self._deps(reads, writes))
        inst = fn(self.eng[e])
        if signal:
            self.val[e] += 1
            inst.then_inc(self.sems[e], 1)
            v = self.val[e]
        else:
            v = self.val[e] + 1
        self._stamp(reads, writes, e, v)

    def dma(self, q, semkey, out, in_, reads, writes, **kw):
        self._sem(semkey)
        self._wait(q, self._deps(reads, writes))
        self.val[semkey] += 16
        self.eng[q].dma_start(out=out, in_=in_, **kw).then_inc(self.sems[semkey], 16)
        self._stamp(reads, writes, semkey, self.val[semkey])

    def barrier(self, engines=None):
        for e in (engines or self.eng):
            self._wait(e, dict(self.val))


def bcast_rows(ap, n=128):
    return bass.AP(tensor=ap.tensor, offset=ap.offset, ap=[[0, n]] + [list(x) for x in ap.ap])


def build_program(groups, depth=2):
    nc = bass.Bass("TRN2", target_bir_lowering=False)
    L = depth
    smax = max(g[3] for g in groups)

    def din(name, shape):
        return nc.dram_tensor(name, list(shape), F32, kind="ExternalInput").ap()

    def dscr(name, shape, dt):
        return nc.dram_tensor(name, list(shape), dt, kind="Internal").ap()

    xin = {g[0]: din(g[0], (g[2], g[3], D)) for g in groups}
    yout = {g[1]: nc.dram_tensor(g[1], [g[2], g[3], D], F32, kind="ExternalOutput").ap() for g in groups}
    P = {n: din(n, s) for n, s in PARAMS}
    oh_in = din("oh_bucket", (32, 768))

    win_b = dscr("win_b", (L, 128, 8, INC), BF16)
    wpa_b = dscr("wpa_b", (L, 128, 8, D), BF16)
    wps_b = dscr("wps_b", (L, 128, 8, D), BF16)
    wo_b = dscr("wo_b", (L, 128, 8, D), BF16)
    wgu_b = dscr("wgu_b", (L, 128, 8, 2 * DFF), BF16)
    wd_b = dscr("wd_b", (L, 128, 22, D), BF16)
    qT_d = dscr("qT_d", (1024, smax), BF16)
    kT_d = dscr("kT_d", (1024, smax), BF16)
    v_d = dscr("v_d", (smax, 1024), BF16)
    sz_d = dscr("sz_d", (smax, 1024), F32)
    xbcT_d = dscr("xbcT_d", (1536, smax), F32)
    dts_d = dscr("dts_d", (smax, 32), F32)
    gT_d = dscr("gT_d", (2048, smax), F32)
    attT_d = dscr("attT_d", (1024, smax), BF16)
    yssmT_d = dscr("yssmT_d", (1024, smax), BF16)
    xst_d = dscr("xst_d", (smax, 1024), F32)
    ysf_d = dscr("ysf_d", (smax, 1024), F32)
    vecs_d = dscr("vecs_d", (8, 768), F32)
    xcur = {g[0]: dscr("xcur_" + g[0], (g[2], g[3], D), F32) for g in groups}

    with ExitStack() as es:
        S = Sched(nc, es)
        es.enter_context(nc.allow_non_contiguous_dma(reason="small param layouts"))
        es.enter_context(nc.allow_low_precision(reason="bf16 matmul operands"))

        uid = [0]

        def mk(stack, name, shape, dt):
            uid[0] += 1
            return stack.enter_context(nc.sbuf_tensor("%s_%d" % (name, uid[0]), list(shape), dt))

        DB = {k: Buf(multi=True) for k in
              ("w", "qT", "kT", "v", "sz", "xbcT", "dts", "gT", "attT", "yssmT", "xst", "ysf", "vecs", "xcur", "in", "out")}

        banks = [es.enter_context(nc.psum_tensor("bank%d" % i, [128, 512], F32)) for i in range(8)]
        bankb = [Buf() for _ in range(8)]

        cst = {}
        cstb = Buf()

        def tri(name, keep, fill, cm, step, base, cmp, dt=F32):
            t = mk(es, "c_" + name, [128, 128], F32)
            S.op("pool", [], [cstb], lambda e: e.memset(t[:], keep))
            S.op("pool", [], [cstb], lambda e: e.affine_select(
                out=t[:], in_=t[:], pattern=[[step, 128]], compare_op=cmp, fill=fill, base=base,
                channel_multiplier=cm))
            if dt == BF16:
                tb = mk(es, "cb_" + name, [128, 128], BF16)
                S.op("dve", [cstb], [cstb], lambda e: e.tensor_copy(out=tb[:], in_=t[:]))
                cst[name + "_bf"] = tb
            cst[name + "_f"] = t

        tri("ident", 0.0, 1.0, 1, -1, 0, ALU.not_equal, BF16)
        tri("U", 1.0, 0.0, -1, 1, 0, ALU.is_ge, BF16)
        tri("Lo", 1.0, 0.0, 1, -1, 0, ALU.is_ge, BF16)
        tri("SL", 1.0, 0.0, 1, -1, -1, ALU.is_ge)
        tri("SU", 1.0, 0.0, -1, 1, -1, ALU.is_ge)
        tri("negU", -1.0, 0.0, -1, 1, 0, ALU.is_ge, BF16)
        tri("negL", -1.0, 0.0, 1, -1, 0, ALU.is_ge, BF16)
        tri("NMf", NEG, 0.0, 1, -1, -1, ALU.is_ge, BF16)
        tri("NMb", NEG, 0.0, -1, 1, -1, ALU.is_ge, BF16)
        tri("J", 0.0, 1.0, 1, 1, -127, ALU.not_equal)
        ones_f = mk(es, "ones_f", [128, 128], F32)
        ones_bf = mk(es, "ones_bf", [128, 128], BF16)
        S.op("dve", [], [cstb], lambda e: e.memset(ones_f[:], 1.0))
        S.op("dve", [], [cstb], lambda e: e.memset(ones_bf[:], 1.0))
        ident_bf, ident_f = cst["ident_bf"], cst["ident_f"]

        for l in range(L):
            for src, dst, kcs in ((P["w_in"], win_b, 8), (P["w_proj_attn"], wpa_b, 8), (P["w_proj_ssm"], wps_b, 8),
                                  (P["w_out"], wo_b, 8), (P["w_gate_up"], wgu_b, 8), (P["w_down"], wd_b, 22)):
                for kc in range(kcs):
                    S.dma("pool", "wcast", dst[l, :, kc, :], src[l, kc * 128:(kc + 1) * 128, :], [DB["in"]], [DB["w"]], max_dma_last_dim=4096)

        Big = mk(es, "Big", [128, 8, 640], BF16)
        cfar = mk(es, "cfar", [128, 16], F32)
        bigb = Buf()
        with ExitStack() as ps:
            tab = mk(ps, "tab", [32, 8], F32)
            oh = mk(ps, "oh", [32, 768], F32)
            vsb = mk(ps, "vsb", [8, 768], F32)
            hk = mk(ps, "hk", [128, 640], F32)
            tb_, ob_, vb_, hb_ = Buf(), Buf(), Buf(), Buf()
            S.dma("sp", "b0", tab[:], P["rel_bias"], [DB["in"]], [tb_])
            S.dma("sp", "b1", oh[:], oh_in, [DB["in"]], [ob_])
            S.dma("sp", "b2", cfar[:, 0:8], bcast_rows(P["rel_bias"][15]), [DB["in"]], [bigb])
            S.dma("sp", "b2", cfar[:, 8:16], bcast_rows(P["rel_bias"][31]), [DB["in"]], [bigb])
            for c0 in (0, 384):
                S.op("pe", [tb_, ob_], [bankb[0]], lambda e: e.matmul(
                    banks[0][0:8, 0:384], lhsT=tab[:], rhs=oh[:, c0:c0 + 384], start=True, stop=True))
                S.op("dve", [bankb[0]], [vb_], lambda e: e.tensor_copy(out=vsb[:, c0:c0 + 384], in_=banks[0][0:8, 0:384]))
            S.dma("pool", "b3", vecs_d, vsb[:], [vb_], [DB["vecs"]])
            for h in range(8):
                hank = bass.AP(tensor=vecs_d.tensor, offset=vecs_d.offset + h * 768, ap=[[1, 128], [1, 640]])
                S.dma("sp", "b4", hk[:], hank, [DB["vecs"]], [hb_])
                for c0 in (0, 320):
                    S.op("pe", [hb_, cstb], [bankb[1]], lambda e: e.matmul(
                        banks[1][:, 0:320], lhsT=cst["J_f"][:], rhs=hk[:, c0:c0 + 320], start=True, stop=True))
                    S.op("dve", [bankb[1]], [bigb], lambda e: e.tensor_copy(out=Big[:, h, c0:c0 + 320], in_=banks[1][:, 0:320]))
            S.barrier()

        lp = {}
        lpb = Buf()
        for nm in ("wpre", "wssm", "wpm", "wpf", "wpo"):
            lp[nm] = mk(es, "lp_" + nm, [128, 1024], F32)
        lp["cw"] = mk(es, "lp_cw", [128, 5, 12], F32)
        lp["cb"] = mk(es, "lp_cb", [128, 12], F32)
        lp["dtb"] = mk(es, "lp_dtb", [128, 32], F32)
        lp["A"] = mk(es, "lp_A", [128, 32], F32)
        lp["Dsk"] = mk(es, "lp_D", [128, 16], F32)
        lp["wsub"] = mk(es, "lp_wsub", [128, 1], F32)
        lp["neglam"] = mk(es, "lp_neglam", [128, 1], F32)
        lp["lam4"] = mk(es, "lp_lam4", [128, 4, 64], F32)
        lp["lamp"] = mk(es, "lp_lamp", [128, 2, 64], F32)
        lp["lams"] = mk(es, "lp_lams", [128, 2], F32)

        def load_layer_params(l):
            S.barrier()
            for nm, key in (("wpre", "norm_pre_mix"), ("wssm", "ssm_norm_w"), ("wpm", "norm_post_mix"),
                            ("wpf", "norm_pre_ffn"), ("wpo", "norm_post_ffn")):
                S.dma("sp", "lp", lp[nm][:], bcast_rows(P[key][l]), [DB["in"]], [lpb])
            for jj in range(5):
                S.dma("sp", "lp", lp["cw"][:, jj, :], P["conv_w"][l][jj, 0].rearrange("(cc p) -> p cc", p=128), [DB["in"]], [lpb])
            S.dma("sp", "lp", lp["cb"][:], P["conv_b"][l].rearrange("(cc p) -> p cc", p=128), [DB["in"]], [lpb])
            S.dma("sp", "lp", lp["dtb"][:, 0:16], bcast_rows(P["dt_bias_f"][l]), [DB["in"]], [lpb])
            S.dma("sp", "lp", lp["dtb"][:, 16:32], bcast_rows(P["dt_bias_b"][l]), [DB["in"]], [lpb])
            S.dma("sp", "lp", lp["A"][:, 0:16], bcast_rows(P["a_log_f"][l]), [DB["in"]], [lpb])
            S.dma("sp", "lp", lp["A"][:, 16:32], bcast_rows(P["a_log_b"][l]), [DB["in"]], [lpb])
            S.dma("sp", "lp", lp["Dsk"][:], bcast_rows(P["d_skip"][l]), [DB["in"]], [lpb])
            S.dma("sp", "lp", lp["wsub"][:], P["subln_w"][l].rearrange("(p o) -> p o", o=1), [DB["in"]], [lpb])
            for i, key in enumerate(("lambda_q1", "lambda_k1", "lambda_q2", "lambda_k2")):
                S.dma("sp", "lp", lp["lam4"][:, i, :], bcast_rows(P[key][l]), [DB["in"]], [lpb])
            lam_init = 0.8 - 0.6 * math.exp(-0.3 * l)
            S.op("act", [lpb], [lpb], lambda e: e.activation(out=lp["A"][:], in_=lp["A"][:], func=AF.Exp))
            S.op("dve", [lpb], [lpb], lambda e: e.tensor_scalar(out=lp["A"][:], in0=lp["A"][:], scalar1=-1.0, scalar2=None, op0=ALU.mult))
            S.op("dve", [lpb], [lpb], lambda e: e.tensor_scalar(out=lp["wsub"][:], in0=lp["wsub"][:], scalar1=1.0 - lam_init, scalar2=None, op0=ALU.mult))
            S.op("dve", [lpb], [lpb], lambda e: e.tensor_tensor(out=lp["lamp"][:, 0, :], in0=lp["lam4"][:, 0, :], in1=lp["lam4"][:, 1, :], op=ALU.mult))
            S.op("dve", [lpb], [lpb], lambda e: e.tensor_tensor(out=lp["lamp"][:, 1, :], in0=lp["lam4"][:, 2, :], in1=lp["lam4"][:, 3, :], op=ALU.mult))
            S.op("dve", [lpb], [lpb], lambda e: e.tensor_reduce(out=lp["lams"][:], in_=lp["lamp"][:], op=ALU.add, axis=mybir.AxisListType.X))
            S.op("act", [lpb], [lpb], lambda e: e.activation(out=lp["lams"][:], in_=lp["lams"][:], func=AF.Exp))
            S.op("dve", [lpb], [lpb], lambda e: e.tensor_tensor(out=lp["neglam"][:], in0=lp["lams"][:, 1:2], in1=lp["lams"][:, 0:1], op=ALU.subtract))
            S.op("dve", [lpb], [lpb], lambda e: e.tensor_scalar(out=lp["neglam"][:], in0=lp["neglam"][:], scalar1=-lam_init, scalar2=None, op0=ALU.add))
            S.barrier()

        import os
        STOP = os.environ.get("KSTOP", "")

        def rms_rstd(stack, tag, n, k=2):
            sq = mk(stack, tag + "_sq", [128, n], F32)
            ss = [mk(stack, tag + "_ss%d" % i, [128, 1], F32) for i in range(k)]
            ln = [mk(stack, tag + "_ln%d" % i, [128, 1], F32) for i in range(k)]
            rs = [mk(stack, tag + "_rs%d" % i, [128, 1], F32) for i in range(k)]
            sqb = Buf()
            ssb = [Buf() for _ in range(k)]
            lnb = [Buf() for _ in range(k)]
            rsb = [Buf() for _ in range(k)]

            def run(src, src_bufs, i):
                S.op("act", src_bufs, [sqb, ssb[i]], lambda e: e.activation(out=sq[:], in_=src, func=AF.Square, accum_out=ss[i][:]))
                S.op("act", [ssb[i]], [lnb[i]], lambda e: e.activation(out=ln[i][:], in_=ss[i][:], func=AF.Ln, scale=1.0 / n, bias=EPS))
                S.op("act", [lnb[i]], [rsb[i]], lambda e: e.activation(out=rs[i][:], in_=ln[i][:], func=AF.Exp, scale=-0.5))
                return rs[i], rsb[i]
            return run

        def mm(out, lhsT, rhs, start, stop):
            return lambda e: e.matmul(out, lhsT=lhsT, rhs=rhs, start=start, stop=stop)

        def phaseA(l, xsrc, xsrc_buf, Sq):
            NT = Sq // 128
            NB = Sq // 512
            with ExitStack() as ps:
                hT = mk(ps, "A_hT", [128, 8, Sq], BF16)
                hTb = [Buf() for _ in range(NB)]
                xt = [mk(ps, "A_xt%d" % i, [128, 1024], F32) for i in range(4)]
                xtb = [Buf() for _ in range(4)]
                hb = [mk(ps, "A_hb%d" % i, [128, 1024], BF16) for i in range(4)]
                hbb = [Buf() for _ in range(4)]
                rms = rms_rstd(ps, "A_r", 1024, 4)
                wr = [mk(ps, "A_w%d" % i, [128, 8, 512], BF16) for i in range(2)]
                wrb = [Buf(), Buf()]
                sbf = [mk(ps, "A_sb%d" % i, [128, 512], BF16) for i in range(4)]
                sbfb = [Buf() for _ in range(4)]
                sf = [mk(ps, "A_sf%d" % i, [128, 512], F32) for i in range(4)]
                sfb = [Buf() for _ in range(4)]
                dtt = [mk(ps, "A_dt%d" % i, [128, 32], F32) for i in range(2)]
                dttb = [Buf(), Buf()]
                trp = banks[7].bitcast(BF16)
                def pro1(t):
                    i = t % 4
                    S.dma("sp", ("A_xt", i), xt[i][:], xsrc[t * 128:(t + 1) * 128, :], [xsrc_buf], [xtb[i]])
                    rs, rsb = rms(xt[i][:], [xtb[i]], i)
                    S.op("dve", [xtb[i], rsb, lpb], [hbb[i]], lambda e: e.scalar_tensor_tensor(
                        out=hb[i][:], in0=xt[i][:], scalar=rs[:, 0:1], in1=lp["wpre"][:], op0=ALU.mult, op1=ALU.mult))

                def pro2(t):
                    i = t % 4
                    for kc in range(8):
                        S.op("pe", [hbb[i], cstb], [bankb[7]], lambda e: e.transpose(
                            out=trp[:, kc * 128:(kc + 1) * 128], in_=hb[i][:, kc * 128:(kc + 1) * 128], identity=ident_bf[:]),
                            signal=(kc == 7))
                    S.op("dve", [bankb[7]], [hTb[t // 4]], lambda e: e.tensor_copy(
                        out=hT[:, :, t * 128:(t + 1) * 128], in_=trp.rearrange("p (k t) -> p k t", k=8)))
                for t in range(min(4, NT)):
                    pro1(t)
                for t in range(min(4, NT)):
                    pro2(t)
                cnt = 0
                for gi, (c0, n, kind) in enumerate(GROUPS):
                    wi = gi % 2
                    S.dma("sp", ("A_w", wi), wr[wi][:, :, 0:n], win_b[l, :, :, c0:c0 + n], [DB["w"]], [wrb[wi]])
                    for tt in range(NB):
                        if gi == 0 and tt + 1 < NB:
                            for t in range((tt + 1) * 4, (tt + 2) * 4):
                                pro1(t)
                        for c4 in range(4 if kind != "dt" else 4):
                            bk = cnt % 4
                            cnt += 1
                            fm = kind in ("q", "k", "x", "g")
                            tok = slice(tt * 512 + c4 * 128, tt * 512 + (c4 + 1) * 128)
                            for kc in range(8):
                                if fm:
                                    f = mm(banks[bk][:, :], wr[wi][:, kc, c4 * 128:(c4 + 1) * 128], hT[:, kc, tt * 512:(tt + 1) * 512], kc == 0, kc == 7)
                                else:
                                    f = mm(banks[bk][:, 0:n], hT[:, kc, tok], wr[wi][:, kc, 0:n], kc == 0, kc == 7)
                                S.op("pe", [wrb[wi], hTb[tt]], [bankb[bk]], f, signal=(kc == 7))
                            si = cnt % 4
                            ev = "act" if (cnt % 2 == 0) else "dve"
                            tsl = slice(tt * 512, (tt + 1) * 512)
                            if kind == "q":
                                S.op("act", [bankb[bk]], [sbfb[si]], lambda e: e.activation(out=sbf[si][:], in_=banks[bk][:], func=AF.Copy, scale=0.125))
                                r0 = c0 + c4 * 128
                                S.dma("pool", ("A_sb", si), qT_d[r0:r0 + 128, tsl], sbf[si][:], [sbfb[si]], [DB["qT"]])
                            elif kind == "k":
                                S.op("dve", [bankb[bk]], [sbfb[si]], lambda e: e.tensor_copy(out=sbf[si][:], in_=banks[bk][:]))
                                r0 = c0 - 1024 + c4 * 128
                                S.dma("pool", ("A_sb", si), kT_d[r0:r0 + 128, tsl], sbf[si][:], [sbfb[si]], [DB["kT"]])
                            elif kind == "x":
                                if ev == "act":
                                    S.op("act", [bankb[bk]], [sfb[si]], lambda e: e.activation(out=sf[si][:], in_=banks[bk][:], func=AF.Copy))
                                else:
                                    S.op("dve", [bankb[bk]], [sfb[si]], lambda e: e.tensor_copy(out=sf[si][:], in_=banks[bk][:]))
                                r0 = c0 - 4096 + c4 * 128
                                S.dma("pool", ("A_sf", si), xbcT_d[r0:r0 + 128, tsl], sf[si][:], [sfb[si]], [DB["xbcT"]])
                            elif kind == "g":
                                S.op("act", [bankb[bk]], [sfb[si]], lambda e: e.activation(out=sf[si][:], in_=banks[bk][:], func=AF.Sigmoid))
                                r0 = c0 - 5664 + c4 * 128
                                S.dma("pool", ("A_sf", si), gT_d[r0:r0 + 128, tsl], sf[si][:], [sfb[si]], [DB["gT"]])
                            elif kind == "v":
                                S.op("dve", [bankb[bk]], [sbfb[si]], lambda e: e.tensor_copy(out=sbf[si][:], in_=banks[bk][:]))
                                S.dma("pool", ("A_sb", si), v_d[tok, c0 - 2048:c0 - 2048 + 512], sbf[si][:], [sbfb[si]], [DB["v"]])
                            elif kind == "z":
                                S.op("act", [bankb[bk]], [sfb[si]], lambda e: e.activation(out=sf[si][:], in_=banks[bk][:], func=AF.Silu))
                                S.dma("pool", ("A_sf", si), sz_d[tok, c0 - 3072:c0 - 3072 + 512], sf[si][:], [sfb[si]], [DB["sz"]])
                            else:
                                di = cnt % 2
                                S.op("dve", [bankb[bk], lpb], [dttb[di]], lambda e: e.tensor_tensor(out=dtt[di][:], in0=banks[bk][:, 0:32], in1=lp["dtb"][:], op=ALU.add))
                                S.op("act", [dttb[di]], [dttb[di]], lambda e: e.activation(out=dtt[di][:], in_=dtt[di][:], func=AF.Exp))
                                S.op("act", [dttb[di]], [dttb[di]], lambda e: e.activation(out=dtt[di][:], in_=dtt[di][:], func=AF.Ln, bias=1.0))
                                S.dma("pool", ("A_dt", di), dts_d[tok, :], dtt[di][:], [dttb[di]], [DB["dts"]])
                        if gi == 0 and tt + 1 < NB:
                            for t in range((tt + 1) * 4, (tt + 2) * 4):
                                pro2(t)
                S.barrier()

        def phaseB(l, Sq):
            NQ = Sq // 256
            NK = Sq // 128
            NSC, NPT, LAG, DEFER = 3, 6, 3, 6
            with ExitStack() as ps:
                q0 = [mk(ps, "B_q0%d" % i, [128, Sq], BF16) for i in range(2)]
                q1 = [mk(ps, "B_q1%d" % i, [128, Sq], BF16) for i in range(2)]
                kT = [mk(ps, "B_k%d" % i, [128, Sq], BF16) for i in range(2)]
                vv = [mk(ps, "B_v%d" % i, [128, NK, 128], BF16) for i in range(2)]
                inb = [Buf(), Buf()]
                PT = [mk(ps, "B_pt%d" % i, [128, 512], BF16) for i in range(NPT)]
                ptb = [Buf() for _ in range(NPT)]
                rinv = mk(ps, "B_rinv", [128, 512], F32)
                On = mk(ps, "B_on", [128, 512], F32)
                dd = [mk(ps, "B_d%d" % i, [128, 256], F32) for i in range(2)]
                sq = [mk(ps, "B_sq%d" % i, [128, 256], F32) for i in range(2)]
                lnv = mk(ps, "B_ln", [128, 256], F32)
                rst = mk(ps, "B_rs", [128, 256], F32)
                att = [mk(ps, "B_att%d" % i, [128, 256], BF16) for i in range(2)]
                fb = Buf()
                f2b = [Buf(), Buf()]
                gb = Buf()
                attb = [Buf(), Buf()]
                for i in range(2):
                    S.op("pool", [], [inb[i]], lambda e: e.memset(q0[i][64:128, :], 0.0))
                    S.op("pool", [], [inb[i]], lambda e: e.memset(q1[i][0:64, :], 0.0))
                OT = [banks[3], banks[4]]
                otb = [bankb[3], bankb[4]]
                SS = [banks[5], banks[6]]
                ssb = [bankb[5], bankb[6]]
                nfin = [0]
                gu = [0]
                pending = []

                def flush(force=False):
                    while pending and (force or pending[0][0] <= gu[0]):
                        pending.pop(0)[1]()

                for h in range(NHA):
                    i = h % 2
                    sk = ("B_in", i)
                    S.dma("sp", sk, q0[i][0:64, :], qT_d[h * 128:h * 128 + 64, 0:Sq], [DB["qT"]], [inb[i]])
                    S.dma("sp", sk, q1[i][64:128, :], qT_d[h * 128 + 64:(h + 1) * 128, 0:Sq], [DB["qT"]], [inb[i]])
                    S.dma("sp", sk, kT[i][:], kT_d[h * 128:(h + 1) * 128, 0:Sq], [DB["kT"]], [inb[i]])
                    for c in range(0, NK, 8):
                        ce = min(NK, c + 8)
                        S.dma("sp", sk, vv[i][:, c:ce, :], v_d[c * 128:ce * 128, h * 128:(h + 1) * 128].rearrange("(c p) e -> p c e", p=128), [DB["v"]], [inb[i]])
                    units = [(j, k) for j in range(NQ) for k in range(NK)]

                    def qk(u, h=h, i=i):
                        j, k = units[u]
                        s = (gu[0]) % NSC
                        p = (gu[0]) % NPT
                        delta = k * 128 - j * 256
                        near = -218 < delta < 346
                        S.op("pe", [inb[i]], [bankb[s]], mm(banks[s][:, 0:256], kT[i][:, k * 128:(k + 1) * 128], q0[i][:, j * 256:(j + 1) * 256], True, False), signal=False)
                        S.op("pe", [inb[i]], [bankb[s]], mm(banks[s][:, 256:512], kT[i][:, k * 128:(k + 1) * 128], q1[i][:, j * 256:(j + 1) * 256], False, not near), signal=not near)
                        if near:
                            c0 = 256 - delta
                            rb = Big[:, h, c0:c0 + 256].unsqueeze(1).to_broadcast([128, 2, 256])
                            S.op("pe", [bigb, cstb], [bankb[s]], mm(banks[s][:].rearrange("p (m q) -> p m q", m=2), ident_bf[:], rb, False, True))
                            S.op("act", [bankb[s]], [ptb[p]], lambda e: e.activation(out=PT[p][:], in_=banks[s][:], func=AF.Exp))
                        else:
                            col = h if delta < 0 else 8 + h
                            S.op("act", [bankb[s], bigb], [ptb[p]], lambda e: e.activation(out=PT[p][:], in_=banks[s][:], func=AF.Exp, bias=cfar[:, col:col + 1]))
                        return p

                    def pv(u, p, h=h, i=i):
                        j, k = units[u]
                        o = j % 2
                        S.op("pe", [inb[i], ptb[p]], [otb[o]], mm(OT[o][:], vv[i][:, k, :], PT[p][:], k == 0, k == NK - 1), signal=False)
                        S.op("pe", [cstb, ptb[p]], [ssb[o]], mm(SS[o][:], ones_bf[:], PT[p][:], k == 0, k == NK - 1))
                        if k == NK - 1:
                            a = nfin[0] % 2
                            nfin[0] += 1
                            S.op("dve", [ssb[o]], [fb], lambda e: e.reciprocal(out=rinv[:], in_=SS[o][:]))
                            S.op("dve", [otb[o], fb], [fb], lambda e: e.tensor_tensor(out=On[:], in0=OT[o][:], in1=rinv[:], op=ALU.mult))
                            S.op("dve", [fb, lpb], [f2b[a]], lambda e: e.scalar_tensor_tensor(out=dd[a][:], in0=On[:, 256:512], scalar=lp["neglam"][:, 0:1], in1=On[:, 0:256], op0=ALU.mult, op1=ALU.add))
                            S.op("dve", [f2b[a]], [f2b[a]], lambda e: e.tensor_tensor(out=sq[a][:], in0=dd[a][:], in1=dd[a][:], op=ALU.mult))

                            def part2(a=a, h=h, j=j):
                                S.op("pe", [f2b[a], cstb], [bankb[7]], mm(banks[7][:, 0:256], ones_f[:], sq[a][:], True, True))
                                S.op("act", [bankb[7]], [gb], lambda e: e.activation(out=lnv[:], in_=banks[7][:, 0:256], func=AF.Ln, scale=1.0 / 128, bias=EPS))
                                S.op("act", [gb], [gb], lambda e: e.activation(out=rst[:], in_=lnv[:], func=AF.Exp, scale=-0.5))
                                S.op("dve", [f2b[a], gb, lpb], [attb[a]], lambda e: e.scalar_tensor_tensor(out=att[a][:], in0=dd[a][:], scalar=lp["wsub"][:, 0:1], in1=rst[:], op0=ALU.mult, op1=ALU.mult))
                                S.dma("pool", ("B_att", a), attT_d[h * 128:(h + 1) * 128, j * 256:(j + 1) * 256], att[a][:], [attb[a]], [DB["attT"]])
                            pending.append((gu[0] + DEFER, part2))

                    nu = len(units)
                    pslots = {}
                    for u in range(nu + LAG):
                        if u < nu:
                            pslots[u] = qk(u)
                        if u >= LAG:
                            pv(u - LAG, pslots.pop(u - LAG))
                        gu[0] += 1
                        flush()
                flush(force=True)
                S.barrier()

        def phaseC(l, Sq):
            NC = Sq // 128
            NB = Sq // 512
            with ExitStack() as ps:
                BT = mk(ps, "C_BT", [128, 4, Sq], BF16)
                Btok = mk(ps, "C_Bt", [128, NC, 256], BF16)
                bcb = Buf()
                with ExitStack() as p0:
                    xin_t = [mk(p0, "C_xi%d" % i, [128, 516], F32) for i in range(4)]
                    xib = [Buf() for _ in range(4)]
                    acc = [mk(p0, "C_ac%d" % i, [128, 512], F32) for i in range(4)]
                    accb = [Buf() for _ in range(4)]
                    xo = [mk(p0, "C_xo%d" % i, [128, 512], F32) for i in range(4)]
                    xob = [Buf() for _ in range(4)]
                    xtk = [mk(p0, "C_xk%d" % i, [128, 512], F32) for i in range(4)]
                    xtkb = [Buf() for _ in range(4)]
                    pc = 0
                    trb = banks[7].bitcast(BF16)
                    for tb in range(NB):
                        lo = tb * 512 - 2
                        hi = tb * 512 + 514
                        slo, shi = max(lo, 0), min(hi, Sq)
                        for cp in range(6):
                            ids = [(pc % 2) * 2 + m for m in range(2)]
                            ccs = [cp * 2 + m for m in range(2)]
                            pc += 1
                            for ix, cc in zip(ids, ccs):
                                if lo < 0 or hi > Sq:
                                    S.op("pool", [], [xib[ix]], lambda e: e.memset(xin_t[ix][:], 0.0))
                                S.dma("sp", ("C_xi", ix), xin_t[ix][:, slo - lo:shi - lo], xbcT_d[cc * 128:(cc + 1) * 128, slo:shi], [DB["xbcT"]], [xib[ix]])
                            for ix, cc in zip(ids, ccs):
                                S.op("dve", [xib[ix], lpb], [accb[ix]], lambda e: e.tensor_scalar(out=acc[ix][:], in0=xin_t[ix][:, 0:512], scalar1=lp["cw"][:, 0, cc:cc + 1], scalar2=None, op0=ALU.mult))
                            for jj in range(1, 5):
                                for ix, cc in zip(ids, ccs):
                                    S.op("dve", [xib[ix], lpb, accb[ix]], [accb[ix]], lambda e: e.scalar_tensor_tensor(
                                        out=acc[ix][:], in0=xin_t[ix][:, jj:jj + 512], scalar=lp["cw"][:, jj, cc:cc + 1], in1=acc[ix][:], op0=ALU.mult, op1=ALU.add))
                            for m, (ix, cc) in enumerate(zip(ids, ccs)):
                                if cc < 8:
                                    S.op("act", [accb[ix], lpb], [xob[ix]], lambda e: e.activation(out=xo[ix][:], in_=acc[ix][:], func=AF.Silu, bias=lp["cb"][:, cc:cc + 1]))
                                    bk = m
                                    for c4 in range(4):
                                        S.op("pe", [xob[ix], cstb], [bankb[bk]], lambda e: e.transpose(
                                            out=banks[bk][:, c4 * 128:(c4 + 1) * 128], in_=xo[ix][:, c4 * 128:(c4 + 1) * 128], identity=ident_f[:]), signal=(c4 == 3))
                                    S.op("act", [bankb[bk]], [xtkb[ix]], lambda e: e.activation(out=xtk[ix][:], in_=banks[bk][:], func=AF.Copy))
                                    S.dma("pool", ("C_xk", ix), xst_d[tb * 512:(tb + 1) * 512, cc * 128:(cc + 1) * 128].rearrange("(c p) e -> p c e", p=128),
                                          xtk[ix][:].rearrange("p (c e) -> p c e", c=4), [xtkb[ix]], [DB["xst"]])
                                else:
                                    S.op("act", [accb[ix], lpb], [bcb], lambda e: e.activation(out=BT[:, cc - 8, tb * 512:(tb + 1) * 512], in_=acc[ix][:], func=AF.Silu, bias=lp["cb"][:, cc:cc + 1]))
                                    if cc < 10:
                                        for c4 in range(4):
                                            S.op("pe", [bcb, cstb], [bankb[7]], lambda e: e.transpose(
                                                out=trb[:, c4 * 128:(c4 + 1) * 128], in_=BT[:, cc - 8, tb * 512 + c4 * 128:tb * 512 + (c4 + 1) * 128], identity=ident_bf[:]), signal=(c4 == 3))
                                        S.op("act", [bankb[7]], [bcb], lambda e: e.activation(
                                            out=Btok[:, tb * 4:(tb + 1) * 4, (cc - 8) * 128:(cc - 7) * 128], in_=trb[:, 0:512].rearrange("p (c e) -> p c e", c=4), func=AF.Copy))
                    S.barrier()
                if STOP == "C0":
                    return
                NBUF = 4

                def many(name, shape, dt, k=NBUF):
                    return [mk(ps, "%s%d" % (name, i), shape, dt) for i in range(k)]

                def bufs(k=NBUF):
                    return [Buf() for _ in range(k)]
                xs = many("C_xs", [128, 1024], F32)
                xsb = bufs()
                dtc = many("C_dt", [128, 32], F32)
                dtcb = bufs()
                ysl = many("C_ys", [128, 1024], F32, 2)
                yslb = bufs(2)
                szl = many("C_sz", [128, 1024], F32, 2)
                szlb = bufs(2)
                a32 = many("C_a32", [128, 16], F32)
                abf = many("C_abf", [128, 16], BF16)
                ab = bufs()
                rhs1 = many("C_rhs1", [128, 16, 128], BF16)
                r1b = [bufs(16) for _ in range(NBUF)]
                G = many("C_G", [128, 16, 128], BF16)
                Gb = [bufs(4) for _ in range(NBUF)]
                cbm = many("C_cbm", [128, 2, 128], BF16)
                cbmb = bufs()
                ex48 = many("C_ex", [128, 48], F32)
                exb = bufs()
                wds = many("C_wds", [128, 16], F32)
                xdt = many("C_xdt", [128, 16, 64], BF16)
                xdtb = bufs()
                xdtd = many("C_xdtd", [128, 16, 64], BF16)
                xdtdb = bufs()
                tmp = mk(ps, "C_tmp", [128, 1024], F32)
                tmpb = [Buf(), Buf()]
                yac = many("C_y", [128, 1024], F32, 2)
                yacb = bufs(2)
                yhb = [bufs(2), bufs(2)]
                prev32 = many("C_p32", [128, 1024], F32, 2)
                prevbf = many("C_pbf", [128, 1024], BF16, 2)
                p32b = bufs(2)
                pbfb = bufs(2)
                ynb = many("C_yn", [128, 1024], BF16, 2)
                ynbb = bufs(2)
                yT = many("C_yT", [128, 8, 128], BF16, 2)
                yTb = bufs(2)
                rms = rms_rstd(ps, "C_r", 1024)
                trb = banks[7].bitcast(BF16)
                for d in (0, 1):
                    S.op("dve", [], [p32b[d]], lambda e: e.memset(prev32[d][:], 0.0))
                    S.op("dve", [], [pbfb[d]], lambda e: e.memset(prevbf[d][:], 0.0))

                def dirc(d):
                    if d == 0:
                        return cst["U_bf"], cst["negU_bf"], cst["NMf_bf"], cst["U_f"], cst["SL_f"]
                    return cst["Lo_bf"], cst["negL_bf"], cst["NMb_bf"], cst["Lo_f"], cst["SU_f"]

                def st0(d, c, i):
                    tok = slice(c * 128, (c + 1) * 128)
                    dsl = slice(d * 16, (d + 1) * 16)
                    S.dma("sp", ("C_xs", i), xs[i][:], xst_d[tok, :], [DB["xst"]], [xsb[i]])
                    S.dma("sp", ("C_dt", i), dtc[i][:], dts_d[tok, :], [DB["dts"]], [dtcb[i]])
                    S.op("dve", [dtcb[i], lpb], [ab[i]], lambda e: e.tensor_tensor(out=a32[i][:], in0=dtc[i][:, dsl], in1=lp["A"][:, dsl], op=ALU.mult))
                    S.op("dve", [ab[i]], [ab[i]], lambda e: e.tensor_copy(out=abf[i][:], in_=a32[i][:]))

                def st1(d, c, i):
                    TRI, NTRI, NM, CUM, DEC = dirc(d)
                    tok = slice(c * 128, (c + 1) * 128)
                    for hh in range(16):
                        S.op("act", [ab[i], cstb], [r1b[i][hh]], lambda e: e.activation(out=rhs1[i][:, hh, :], in_=TRI[:], func=AF.Copy, scale=a32[i][:, hh:hh + 1]))
                    S.op("pe", [ab[i], cstb], [bankb[2]], mm(banks[2][:, 256:272], CUM[:], a32[i][:], True, True), signal=False)
                    S.op("pe", [ab[i], cstb], [bankb[2]], mm(banks[2][:, 272:288], DEC[:], a32[i][:], True, True), signal=False)
                    S.op("pe", [ab[i], cstb], [bankb[2]], mm(banks[2][:, 288:304], ones_f[:], a32[i][:], True, True), signal=False)
                    for g in range(2):
                        S.op("pe", [bcb], [bankb[2]], mm(banks[2][:, g * 128:(g + 1) * 128], BT[:, g, tok], BT[:, 2 + g, tok], True, True), signal=(g == 1))
                    S.op("act", [bankb[2]], [exb[i]], lambda e: e.activation(out=ex48[i][:], in_=banks[2][:, 256:304], func=AF.Exp))
                    S.op("dve", [bankb[2], cstb, exb[i]], [cbmb[i]], lambda e: e.tensor_tensor(
                        out=cbm[i][:], in0=banks[2][:, 0:256].rearrange("p (g l) -> p g l", g=2), in1=TRI[:].unsqueeze(1).to_broadcast([128, 2, 128]), op=ALU.mult))
                    for hq in range(4):
                        bk = hq % 2
                        ov = banks[bk][:].rearrange("p (h l) -> p h l", h=4)
                        S.op("pe", r1b[i][hq * 4:(hq + 1) * 4] + [cstb], [bankb[bk]], mm(ov, ones_bf[:], rhs1[i][:, hq * 4:(hq + 1) * 4, :], True, False), signal=False)
                        S.op("pe", [ab[i], cstb], [bankb[bk]], mm(ov, NTRI[:], abf[i][:, hq * 4:(hq + 1) * 4].unsqueeze(2).to_broadcast([128, 4, 128]), False, False), signal=False)
                        S.op("pe", [cstb], [bankb[bk]], mm(ov, ident_bf[:], NM[:].unsqueeze(1).to_broadcast([128, 4, 128]), False, True))
                        S.op("act", [bankb[bk]], [Gb[i][hq]], lambda e: e.activation(out=G[i][:, hq * 4:(hq + 1) * 4, :], in_=ov, func=AF.Exp))

                def st2(d, c, i):
                    dsl = slice(d * 16, (d + 1) * 16)
                    xs3 = xs[i][:].rearrange("p (h e) -> p h e", h=16)
                    S.op("dve", [xsb[i], dtcb[i]], [xdtb[i]], lambda e: e.tensor_tensor(
                        out=xdt[i][:], in0=xs3, in1=dtc[i][:, dsl].unsqueeze(2).to_broadcast([128, 16, 64]), op=ALU.mult))
                    S.op("dve", [exb[i], dtcb[i]], [exb[i]], lambda e: e.tensor_tensor(out=wds[i][:], in0=ex48[i][:, 16:32], in1=dtc[i][:, dsl], op=ALU.mult))
                    S.op("dve", [xsb[i], exb[i]], [xdtdb[i]], lambda e: e.tensor_tensor(
                        out=xdtd[i][:], in0=xs3, in1=wds[i][:].unsqueeze(2).to_broadcast([128, 16, 64]), op=ALU.mult))
                    for hq in range(4):
                        S.op("dve", [Gb[i][hq], cbmb[i]], [Gb[i][hq]], lambda e: e.tensor_tensor(
                            out=G[i][:, hq * 4:(hq + 1) * 4, :], in0=G[i][:, hq * 4:(hq + 1) * 4, :], in1=cbm[i][:, hq // 2, :].unsqueeze(1).to_broadcast([128, 4, 128]), op=ALU.mult))

                def back(d, c, i, j):
                    tok = slice(c * 128, (c + 1) * 128)
                    xs3 = xs[i][:].rearrange("p (h e) -> p h e", h=16)
                    if d == 1:
                        S.dma("sp", ("C_ys", j), ysl[j][:], ysf_d[tok, :], [DB["ysf"]], [yslb[j]])
                        S.dma("sp", ("C_sz", j), szl[j][:], sz_d[tok, :], [DB["sz"]], [szlb[j]])
                    for hh in range(16):
                        bk = 3 + hh // 8
                        S.op("pe", [Gb[i][hh // 4], xdtb[i]], [bankb[bk]], mm(banks[bk][:, (hh % 8) * 64:(hh % 8 + 1) * 64], G[i][:, hh, :], xdt[i][:, hh, :], True, True), signal=(hh % 8 == 7))
                    for g in range(2):
                        S.op("pe", [bcb, pbfb[d]], [bankb[5 + g]], mm(banks[5 + g][:], BT[:, 2 + g, tok], prevbf[d][:, g * 512:(g + 1) * 512], True, True))
                    for g in range(2):
                        S.op("dve", [bankb[5 + g], exb[i]], [tmpb[g]], lambda e: e.tensor_tensor(
                            out=tmp[:, g * 512:(g + 1) * 512].rearrange("p (h e) -> p h e", h=8), in0=banks[5 + g][:].rearrange("p (h e) -> p h e", h=8),
                            in1=ex48[i][:, g * 8:(g + 1) * 8].unsqueeze(2).to_broadcast([128, 8, 64]), op=ALU.mult))
                    for g in range(2):
                        S.op("pe", [bcb, xdtdb[i]], [bankb[5 + g]], mm(banks[5 + g][:], Btok[:, c, g * 128:(g + 1) * 128], xdtd[i][:, g * 8:(g + 1) * 8, :], True, True))
                    S.op("dve", [p32b[d], exb[i]], [p32b[d]], lambda e: e.tensor_tensor(
                        out=prev32[d][:].rearrange("p (h e) -> p h e", h=16), in0=prev32[d][:].rearrange("p (h e) -> p h e", h=16),
                        in1=ex48[i][:, 32:48].unsqueeze(2).to_broadcast([128, 16, 64]), op=ALU.mult))
                    for g in range(2):
                        S.op("dve", [p32b[d], bankb[5 + g]], [p32b[d]], lambda e: e.tensor_tensor(
                            out=prev32[d][:, g * 512:(g + 1) * 512], in0=banks[5 + g][:], in1=prev32[d][:, g * 512:(g + 1) * 512], op=ALU.add))
                    S.op("act", [p32b[d]], [pbfb[d]], lambda e: e.activation(out=prevbf[d][:], in_=prev32[d][:], func=AF.Copy))
                    yi = yac[j]
                    for g in range(2):
                        S.op("dve", [bankb[3 + g], tmpb[g]], [yhb[j][g], yacb[j]], lambda e: e.tensor_tensor(
                            out=yi[:, g * 512:(g + 1) * 512], in0=banks[3 + g][:], in1=tmp[:, g * 512:(g + 1) * 512], op=ALU.add))
                    if d == 0:
                        S.op("dve", [xsb[i], lpb], tmpb, lambda e: e.tensor_tensor(
                            out=tmp[:].rearrange("p (h e) -> p h e", h=16), in0=xs3, in1=lp["Dsk"][:].unsqueeze(2).to_broadcast([128, 16, 64]), op=ALU.mult))
                        S.op("dve", tmpb + yhb[j], [yacb[j]], lambda e: e.tensor_tensor(out=yi[:], in0=yi[:], in1=tmp[:], op=ALU.add))
                        S.dma("pool", ("C_y", j), ysf_d[tok, :], yi[:], [yacb[j]], [DB["ysf"]])
                    else:
                        S.op("dve", [yslb[j]] + yhb[j], [yacb[j]], lambda e: e.tensor_tensor(out=yi[:], in0=yi[:], in1=ysl[j][:], op=ALU.add))
                        S.op("dve", [szlb[j], yacb[j]], [yacb[j]], lambda e: e.tensor_tensor(out=yi[:], in0=yi[:], in1=szl[j][:], op=ALU.mult))
                        rs, rsb = rms(yi[:], [yacb[j]], j)
                        S.op("dve", [yacb[j], rsb, lpb], [ynbb[j]], lambda e: e.scalar_tensor_tensor(
                            out=ynb[j][:], in0=yi[:], scalar=rs[:, 0:1], in1=lp["wssm"][:], op0=ALU.mult, op1=ALU.mult))
                        for kc in range(8):
                            S.op("pe", [ynbb[j], cstb], [bankb[7]], lambda e: e.transpose(
                                out=trb[:, kc * 128:(kc + 1) * 128], in_=ynb[j][:, kc * 128:(kc + 1) * 128], identity=ident_bf[:]), signal=(kc == 7))
                        S.op("act", [bankb[7]], [yTb[j]], lambda e: e.activation(out=yT[j][:], in_=trb.rearrange("p (k t) -> p k t", k=8), func=AF.Copy))
                        S.dma("pool", ("C_yT", j), yssmT_d.rearrange("(k p) s -> p k s", p=128)[:, :, tok], yT[j][:], [yTb[j]], [DB["yssmT"]])

                steps = [(0, c) for c in range(NC)] + [(1, c) for c in range(NC - 1, -1, -1)]
                ns = len(steps)
                for n in range(-3, ns):
                    for off, fn in ((3, st0), (2, st1), (1, st2)):
                        m = n + off
                        if 0 <= m < ns and (off == 3 or True):
                            if (off == 3) or (off == 2 and m >= 0) or (off == 1 and m >= 0):
                                fn(steps[m][0], steps[m][1], m % NBUF)
                    if n >= 0:
                        back(steps[n][0], steps[n][1], n % NBUF, n % 2)
                S.barrier()

        def phaseD(l, xsrc, xsrc_buf, xdst, xdst_buf, Sq):
            NB = Sq // 512
            with ExitStack() as ps:
                aT = mk(ps, "D_aT", [128, 8, 512], BF16)
                sT = mk(ps, "D_sT", [128, 8, 512], BF16)
                aTb, sTb = Buf(), Buf()
                gt = [mk(ps, "D_g%d" % i, [128, 2, 512], F32) for i in range(2)]
                gtb = [Buf(), Buf()]
                t1 = [mk(ps, "D_t1%d" % i, [128, 512], F32) for i in range(2)]
                t1b = [Buf(), Buf()]
                mT = mk(ps, "D_mT", [128, 8, 512], BF16)
                mTb = [Buf() for _ in range(8)]
                xn = mk(ps, "D_xn", [128, 4, 1024], F32)
                xnb = [Buf() for _ in range(4)]
                mm_sb = [mk(ps, "D_m%d" % i, [128, 1024], F32) for i in range(2)]
                mmb = [Buf(), Buf()]
                hb = [mk(ps, "D_hb%d" % i, [128, 1024], BF16) for i in range(2)]
                hbb = [Buf(), Buf()]
                h2T = mk(ps, "D_h2T", [128, 8, 512], BF16)
                h2Tb = [Buf() for _ in range(4)]
                fT = mk(ps, "D_fT", [128, 22, 512], BF16)
                fTb = [Buf() for _ in range(22)]
                sg = [mk(ps, "D_sg%d" % i, [128, 512], F32) for i in range(2)]
                sgb = [Buf(), Buf()]
                wr = [mk(ps, "D_w%d" % i, [128, 22, 256], BF16) for i in range(4)]
                wrb = [Buf() for _ in range(4)]
                wr8 = [mk(ps, "D_w8%d" % i, [128, 8, 256], BF16) for i in range(4)]
                wr8b = [Buf() for _ in range(4)]
                rms = rms_rstd(ps, "D_r", 1024)
                trp = banks[7].bitcast(BF16)
                wcnt = [0, 0]

                def wload(src_ap, kcs, n):
                    if kcs == 8:
                        wi = wcnt[1] % 4
                        wcnt[1] += 1
                        S.dma("sp", ("D_w8", wi), wr8[wi][:, :, 0:n], src_ap, [DB["w"]], [wr8b[wi]])
                        return wr8[wi], wr8b[wi]
                    wi = wcnt[0] % 4
                    wcnt[0] += 1
                    S.dma("sp", ("D_w", wi), wr[wi][:, 0:kcs, 0:n], src_ap, [DB["w"]], [wrb[wi]])
                    return wr[wi], wrb[wi]

                pcnt = [0]

                def nbank():
                    b = pcnt[0] % 6
                    pcnt[0] += 1
                    return b

                def norm_resid(src_sb, src_b, xres_ap, xres_bufs, wkey, out_ap, out_bufs, ri):
                    rs, rsb = rms(src_sb[:], [src_b], ri)
                    S.op("dve", [src_b, rsb, lpb], [src_b], lambda e: e.scalar_tensor_tensor(
                        out=src_sb[:], in0=src_sb[:], scalar=rs[:, 0:1], in1=lp[wkey][:], op0=ALU.mult, op1=ALU.mult))
                    S.op("dve", [src_b] + xres_bufs, out_bufs, lambda e: e.tensor_tensor(out=out_ap, in0=src_sb[:], in1=xres_ap, op=ALU.add))

                for tb in range(NB):
                    tsl = slice(tb * 512, (tb + 1) * 512)
                    S.dma("sp", "D_aT", aT[:], attT_d.rearrange("(k p) s -> p k s", p=128)[:, :, tsl], [DB["attT"]], [aTb])
                    S.dma("sp", "D_sT", sT[:], yssmT_d.rearrange("(k p) s -> p k s", p=128)[:, :, tsl], [DB["yssmT"]], [sTb])
                    for oc in range(8):
                        gi = oc % 2
                        S.dma("sp", ("D_g", gi), gt[gi][:], gT_d.rearrange("(t r) s -> r t s", t=2)[oc * 128:(oc + 1) * 128, :, tsl], [DB["gT"]], [gtb[gi]])
                        if oc % 2 == 0:
                            wa, wab = wload(wpa_b[l, :, :, oc * 128:oc * 128 + 256], 8, 256)
                            ws, wsb = wload(wps_b[l, :, :, oc * 128:oc * 128 + 256], 8, 256)
                        co = (oc % 2) * 128
                        b1, b2 = nbank(), nbank()
                        for kc in range(8):
                            S.op("pe", [wab, aTb], [bankb[b1]], mm(banks[b1][:], wa[:, kc, co:co + 128], aT[:, kc, :], kc == 0, kc == 7), signal=(kc == 7))
                        for kc in range(8):
                            S.op("pe", [wsb, sTb], [bankb[b2]], mm(banks[b2][:], ws[:, kc, co:co + 128], sT[:, kc, :], kc == 0, kc == 7), signal=(kc == 7))
                        S.op("dve", [bankb[b1], gtb[gi]], [t1b[gi]], lambda e: e.tensor_tensor(out=t1[gi][:], in0=banks[b1][:], in1=gt[gi][:, 0, :], op=ALU.mult))
                        S.op("dve", [bankb[b2], gtb[gi]], [gtb[gi]], lambda e: e.tensor_tensor(out=gt[gi][:, 1, :], in0=banks[b2][:], in1=gt[gi][:, 1, :], op=ALU.mult))
                        S.op("dve", [t1b[gi], gtb[gi]], [mTb[oc]], lambda e: e.tensor_tensor(out=mT[:, oc, :], in0=t1[gi][:], in1=gt[gi][:, 1, :], op=ALU.add))
                    for c4 in range(4):
                        S.dma("sp", ("D_xn", c4), xn[:, c4, :], xsrc[tb * 512 + c4 * 128:tb * 512 + (c4 + 1) * 128, :], [xsrc_buf], [xnb[c4]])
                    wos = []
                    for hf in range(4):
                        wos.append(wload(wo_b[l, :, :, hf * 256:(hf + 1) * 256], 8, 256))
                    deferred = None
                    for c4 in range(4):
                        mi = c4 % 2
                        evs = []
                        for hf in range(2):
                            b = nbank()
                            for q2 in range(2):
                                wt, wtb = wos[hf * 2 + q2]
                                for kc in range(8):
                                    S.op("pe", [wtb, mTb[kc]], [bankb[b]], mm(banks[b][:, q2 * 256:(q2 + 1) * 256], mT[:, kc, c4 * 128:(c4 + 1) * 128], wt[:, kc, 0:256], (kc == 0 and q2 == 0), kc == 7 and q2 == 1),
                                         signal=(kc == 7 and q2 == 1))
                            evs.append((b, hf))
                        if deferred:
                            deferred()
                        for b, hf in evs:
                            S.op("act", [bankb[b]], [mmb[mi]], lambda e: e.activation(out=mm_sb[mi][:, hf * 512:(hf + 1) * 512], in_=banks[b][:], func=AF.Copy))
                        norm_resid(mm_sb[mi], mmb[mi], xn[:, c4, :], [xnb[c4]], "wpm", xn[:, c4, :], [xnb[c4]], mi)
                        rs, rsb = rms(xn[:, c4, :], [xnb[c4]], mi)
                        S.op("dve", [xnb[c4], rsb, lpb], [hbb[mi]], lambda e: e.scalar_tensor_tensor(
                            out=hb[mi][:], in0=xn[:, c4, :], scalar=rs[:, 0:1], in1=lp["wpf"][:], op0=ALU.mult, op1=ALU.mult))

                        def deferred(c4=c4, mi=mi):
                            for kc in range(8):
                                S.op("pe", [hbb[mi], cstb], [bankb[7]], lambda e: e.transpose(
                                    out=trp[:, kc * 128:(kc + 1) * 128], in_=hb[mi][:, kc * 128:(kc + 1) * 128], identity=ident_bf[:]), signal=(kc == 7))
                            S.op("dve", [bankb[7]], [h2Tb[c4]], lambda e: e.tensor_copy(
                                out=h2T[:, :, c4 * 128:(c4 + 1) * 128], in_=trp.rearrange("p (k t) -> p k t", k=8)))
                    deferred()
                    wds_ = []
                    for hf in range(4):
                        wds_.append(wload(wd_b[l, :, :, hf * 256:(hf + 1) * 256], 22, 256))
                    for fc in range(22):
                        if fc % 2 == 0:
                            wg, wgb = wload(wgu_b[l, :, :, fc * 128:fc * 128 + 256], 8, 256)
                            wu, wub = wload(wgu_b[l, :, :, DFF + fc * 128:DFF + fc * 128 + 256], 8, 256)
                        co = (fc % 2) * 128
                        b1, b2 = nbank(), nbank()
                        si = fc % 2
                        for kc in range(8):
                            S.op("pe", [wgb] + h2Tb, [bankb[b1]], mm(banks[b1][:], wg[:, kc, co:co + 128], h2T[:, kc, :], kc == 0, kc == 7), signal=(kc == 7))
                        for kc in range(8):
                            S.op("pe", [wub] + h2Tb, [bankb[b2]], mm(banks[b2][:], wu[:, kc, co:co + 128], h2T[:, kc, :], kc == 0, kc == 7), signal=(kc == 7))
                        S.op("act", [bankb[b1]], [sgb[si]], lambda e: e.activation(out=sg[si][:], in_=banks[b1][:], func=AF.Silu))
                        S.op("dve", [bankb[b2], sgb[si]], [fTb[fc]], lambda e: e.tensor_tensor(out=fT[:, fc, :], in0=banks[b2][:], in1=sg[si][:], op=ALU.mult))
                    for c4 in range(4):
                        mi = c4 % 2
                        for hf in range(2):
                            b = nbank()
                            for q2 in range(2):
                                wt, wtb = wds_[hf * 2 + q2]
                                for fc in range(22):
                                    S.op("pe", [wtb, fTb[fc]], [bankb[b]], mm(banks[b][:, q2 * 256:(q2 + 1) * 256], fT[:, fc, c4 * 128:(c4 + 1) * 128], wt[:, fc, 0:256], (fc == 0 and q2 == 0), fc == 21 and q2 == 1),
                                         signal=(fc == 21 and q2 == 1))
                            S.op("act", [bankb[b]], [mmb[mi]], lambda e: e.activation(out=mm_sb[mi][:, hf * 512:(hf + 1) * 512], in_=banks[b][:], func=AF.Copy))
                        norm_resid(mm_sb[mi], mmb[mi], xn[:, c4, :], [xnb[c4]], "wpo", xn[:, c4, :], [xnb[c4]], mi)
                        S.dma("pool", ("D_xo", c4), xdst[tb * 512 + c4 * 128:tb * 512 + (c4 + 1) * 128, :], xn[:, c4, :], [xnb[c4]], [xdst_buf])
                S.barrier()

        S.barrier()
        for l in range(L if STOP != "pre" else 0):
            load_layer_params(l)
            for (iname, oname, nseq, Sq) in groups:
                for si in range(nseq):
                    if l == 0:
                        xsrc, xsb_ = xin[iname][si], DB["in"]
                    else:
                        xsrc, xsb_ = xcur[iname][si], DB["xcur"]
                    if l == L - 1:
                        xdst, xdb_ = yout[oname][si], DB["out"]
                    else:
                        xdst, xdb_ = xcur[iname][si], DB["xcur"]
                    if STOP and l > 0:
                        continue
                    phaseA(l, xsrc, xsb_, Sq)
                    if STOP == "A":
                        continue
                    phaseB(l, Sq)
                    if STOP == "B":
                        continue
                    phaseC(l, Sq)
                    if STOP in ("C", "C0"):
                        continue
                    phaseD(l, xsrc, xsb_, xdst, xdb_, Sq)
        S.barrier()
    return nc


def _bucket_onehot():
    import jax
    import jax.numpy as jnp
    with jax.default_device(jax.devices("cpu")[0]):
        rel = 383 - jnp.arange(768, dtype=jnp.int32)
        nb = 16
        ret = (rel > 0).astype(jnp.int32) * nb
        n = jnp.abs(rel)
        max_exact = nb // 2
        nf = jnp.maximum(n, 1).astype(jnp.float32)
        large = max_exact + (jnp.log(nf / max_exact) / math.log(128 / max_exact) * (nb - max_exact)).astype(jnp.int32)
        large = jnp.minimum(large, nb - 1)
        bk = np.asarray(ret + jnp.where(n < max_exact, n, large))
    oh = np.zeros((32, 768), np.float32)
    oh[bk, np.arange(768)] = 1.0
    return oh


_CACHE = {}


def kernel(**inputs):
    n = 8
    xp = np.ascontiguousarray(inputs["x_prompt"], dtype=np.float32)
    xs = np.ascontiguousarray(inputs["x_sample"], dtype=np.float32)
    bp, sp = xp.shape[0] // n, xp.shape[1]
    bs, ss = xs.shape[0] // n, xs.shape[1]
    groups = [("xp", "yp", bp, sp), ("xs", "ys", bs, ss)]
    key = tuple(groups)
    if key not in _CACHE:
        _CACHE[key] = build_program(groups, depth=2)
    nc = _CACHE[key]
    oh = _bucket_onehot()
    shared = {name: np.ascontiguousarray(inputs[name], dtype=np.float32) for name, _ in PARAMS}
    shared["oh_bucket"] = oh
    in_maps = []
    for c in range(n):
        m = dict(shared)
        m["xp"] = xp[c * bp:(c + 1) * bp]
        m["xs"] = xs[c * bs:(c + 1) * bs]
        in_maps.append(m)
    res = run_bass_kernel_spmd(nc, in_maps, core_ids=list(range(n)))
    yp = np.concatenate([r["yp"] for r in res.results], axis=0)
    ys = np.concatenate([r["ys"] for r in res.results], axis=0)
    return (yp.astype(np.float32), ys.astype(np.float32))
```
